# Optimizing a Trainium2 kernel written in Bass

```python
import math, functools
import jax, jax.numpy as jnp
from jax import lax
import numpy as np

D_MODEL = 1024
BATCH = 2
SEQ = 8192
DEPTH = 1
DEC_BATCH = 128
DEC_SEQ = 1
PAST_LEN = 8192
PAGE_SIZE = 128

WIN_GROUPS = ((128, 1), (512, 4), (2048, 16))
N_GROUPS = 3
A_HEADS = 4
A_HEAD_DIM = D_MODEL // 16
D_A = A_HEADS * A_HEAD_DIM
QBLK = 128
HG_KEY_DIM = 128
HG_VAL_DIM = 128
D_B = D_MODEL // 2
HG_HEADS = D_B // HG_KEY_DIM
HG_CHUNK = 64
MEM_LEN = 256
MEM_HEADS = 4
MEM_HEAD_DIM = D_MODEL // 16
D_M = MEM_HEADS * MEM_HEAD_DIM
N_BRANCH = 3
EPS = 1e-6
IN_SPLITS = (N_GROUPS * D_A, N_GROUPS * D_A, N_GROUPS * D_A, D_A,
             D_B, D_B, D_B, D_B,
             D_M, D_M,
             N_BRANCH * D_MODEL)
D_IN = sum(IN_SPLITS)
F32 = jnp.float32

kernel_name = 'hybrid_dilated_hgrn2_mem_decoder_step'


def _split_points():
    pts, acc = [], 0
    for s in IN_SPLITS[:-1]:
        acc += s
        pts.append(acc)
    return pts


def rms_norm(x, gain):
    x32 = x.astype(F32)
    y = x32 * lax.rsqrt(jnp.mean(x32 * x32, axis=-1, keepdims=True) + EPS)
    return (y * gain.astype(F32)).astype(x.dtype)


def dilated_window_prompt(q, k, v, window, dil):
    B, S, H, Dh = q.shape
    steps = window // dil
    n = S // dil
    nb = -(-n // QBLK)
    pad = nb * QBLK - n

    def blocks(t):
        t = t.astype(F32).reshape(B, n, dil, H, Dh).transpose(0, 2, 1, 3, 4)
        t = jnp.pad(t, ((0, 0), (0, 0), (0, pad), (0, 0), (0, 0)))
        return t.reshape(B, dil, nb, QBLK, H, Dh)

    def with_prev(t):
        prev = jnp.pad(t[:, :, :-1], ((0, 0), (0, 0), (1, 0), (0, 0), (0, 0), (0, 0)))
        return jnp.concatenate([prev, t], axis=3)

    qb = blocks(q)
    kk = with_prev(blocks(k))
    vv = with_prev(blocks(v))
    s = jnp.einsum('brnqhd,brnkhd->brnhqk', qb, kk) * (Dh ** -0.5)
    qi = jnp.arange(QBLK)[:, None]
    kj = jnp.arange(2 * QBLK)[None, :]
    dist = QBLK + qi - kj
    key_idx = (jnp.arange(nb)[:, None, None] - 1) * QBLK + kj[None]
    mask = (dist >= 0) & (dist <= steps) & (key_idx >= 0)
    s = jnp.where(mask[:, None], s, -jnp.inf)
    mx = jnp.max(s, axis=-1, keepdims=True)
    p = jnp.exp(s - mx)
    l = jnp.sum(p, axis=-1, keepdims=True)
    o = jnp.einsum('brnhqk,brnkhd->brnqhd', p, vv)
    o = o / jnp.moveaxis(l[..., 0], 3, 4)[..., None]
    lse = jnp.moveaxis((mx + jnp.log(l))[..., 0], 3, 4)

    def unblock(t):
        t = t.reshape((B, dil, nb * QBLK) + t.shape[4:])[:, :, :n]
        t = jnp.moveaxis(t, 1, 2)
        return t.reshape((B, S) + t.shape[3:])

    return unblock(o), unblock(lse)


def dilated_window_step(q, k, v, kv_buf, window, dil):
    T = q.shape[1]
    L = kv_buf.shape[1]
    Dh = q.shape[-1]
    steps = window // dil
    kc = jnp.concatenate([kv_buf[:, :, 0].astype(F32), k.astype(F32)], axis=1)
    vc = jnp.concatenate([kv_buf[:, :, 1].astype(F32), v.astype(F32)], axis=1)
    idx = L + jnp.arange(T)[:, None] - dil * jnp.arange(steps + 1)[None, :]
    valid = idx >= 0
    idx = jnp.maximum(idx, 0)
    kg = kc[:, idx]
    vg = vc[:, idx]
    s = jnp.einsum('bthd,btmhd->bthm', q.astype(F32), kg) * (Dh ** -0.5)
    s = jnp.where(valid[None, :, None, :], s, -jnp.inf)
    mx = jnp.max(s, axis=-1, keepdims=True)
    p = jnp.exp(s - mx)
    l = jnp.sum(p, axis=-1, keepdims=True)
    o = jnp.einsum('bthm,btmhd->bthd', p, vg) / l
    lse = (mx + jnp.log(l))[..., 0]
    return o, lse


def combine_groups(outs, lses):
    w = jax.nn.softmax(jnp.stack(lses, axis=0), axis=0)
    return jnp.sum(w[..., None] * jnp.stack(outs, axis=0), axis=0)


def window_mixer_prompt(qa, ka, va):
    S = qa.shape[1]
    outs, lses, rows = [], [], []
    for g, (w, d) in enumerate(WIN_GROUPS):
        o, lse = dilated_window_prompt(qa[:, :, g], ka[:, :, g], va[:, :, g], w, d)
        outs.append(o)
        lses.append(lse)
        L = min(w, S)
        rows.append(jnp.stack([ka[:, S - L:, g], va[:, S - L:, g]], axis=2))
    return combine_groups(outs, lses), rows


def window_mixer_step(qa, ka, va, bufs):
    outs, lses, rows = [], [], []
    for g, (w, d) in enumerate(WIN_GROUPS):
        o, lse = dilated_window_step(qa[:, :, g], ka[:, :, g], va[:, :, g], bufs[g], w, d)
        outs.append(o)
        lses.append(lse)
        rows.append(jnp.stack([ka[:, :, g], va[:, :, g]], axis=2))
    return combine_groups(outs, lses), rows


def hgrn2_chunked(q, k, v, g, s0):
    B, L, H, Dk = q.shape
    Dv = v.shape[-1]
    C = min(HG_CHUNK, L)
    nc = -(-L // C)
    pad = nc * C - L

    def chunks(t):
        t = jnp.pad(t.astype(F32), ((0, 0), (0, pad), (0, 0), (0, 0)))
        return t.reshape(B, nc, C, H, t.shape[-1]).transpose(1, 0, 3, 2, 4)

    causal = jnp.tril(jnp.ones((C, C), dtype=bool))[..., None]

    def step(s, inp):
        qc, kc, vc, gc = inp
        G = jnp.cumsum(gc, axis=2)
        o = jnp.einsum('bhtd,bhde->bhte', qc * jnp.exp(G), s)
        diff = G[:, :, :, None, :] - G[:, :, None, :, :]
        decay = jnp.where(causal, jnp.exp(jnp.where(causal, diff, 0.0)), 0.0)
        att = jnp.einsum('bhtsd,bhsd->bhts', qc[:, :, :, None, :] * decay, kc)
        o = o + jnp.einsum('bhts,bhse->bhte', att, vc)
        g_last = G[:, :, -1:, :]
        s = jnp.exp(g_last[:, :, 0, :, None]) * s + jnp.einsum('bhsd,bhse->bhde', kc * jnp.exp(g_last - G), vc)
        return s, o

    s_fin, o = lax.scan(step, s0.astype(F32), (chunks(q), chunks(k), chunks(v), chunks(g)))
    o = o.transpose(1, 0, 3, 2, 4).reshape(B, nc * C, H, Dv)[:, :L]
    return o, s_fin


def memory_kv(mem, gain, w_kv):
    B = mem.shape[0]
    hm = rms_norm(mem, gain)
    return jnp.einsum('bmd,de->bme', hm, w_kv).reshape(B, MEM_LEN, 2, MEM_HEADS, MEM_HEAD_DIM)


def memory_attention(q, mem_kv):
    s = jnp.einsum('bthd,bmhd->bhtm', q.astype(F32), mem_kv[:, :, 0].astype(F32)) * (MEM_HEAD_DIM ** -0.5)
    p = jax.nn.softmax(s, axis=-1)
    return jnp.einsum('bhtm,bmhd->bthd', p, mem_kv[:, :, 1].astype(F32))


def layer_forward(x, win_fn, hg_state0, mem_kv, lb, norm_in_l, w_in_l, norm_hgrn_l,
                  w_branch_a_l, w_branch_b_l, w_branch_m_l, w_out_l):
    B, T, _ = x.shape
    h = rms_norm(x, norm_in_l)
    z = jnp.einsum('btd,de->bte', h, w_in_l)
    qa, ka, va, ga, qb, fb, ib, gb, qm, gm, zg = jnp.split(z, _split_points(), axis=-1)
    grp = lambda t: t.reshape(B, T, N_GROUPS, A_HEADS, A_HEAD_DIM)
    oa, win_rows = win_fn(grp(qa), grp(ka), grp(va))
    ua = oa.reshape(B, T, D_A) * jax.nn.silu(ga.astype(F32))
    f = lb + (1.0 - lb) * jax.nn.sigmoid(fb.astype(F32))
    f = f.reshape(B, T, HG_HEADS, HG_KEY_DIM)
    hq = jax.nn.silu(qb.astype(F32)).reshape(B, T, HG_HEADS, HG_KEY_DIM)
    ob, hg_state = hgrn2_chunked(hq, 1.0 - f, ib.reshape(B, T, HG_HEADS, HG_VAL_DIM), jnp.log(f), hg_state0)
    ub = rms_norm(ob, norm_hgrn_l.reshape(HG_HEADS, HG_VAL_DIM)).reshape(B, T, D_B) * jax.nn.silu(gb.astype(F32))
    om = memory_attention(qm.reshape(B, T, MEM_HEADS, MEM_HEAD_DIM), mem_kv)
    um = om.reshape(B, T, D_M) * jax.nn.silu(gm.astype(F32))
    gates = jax.nn.sigmoid(zg.astype(F32)).reshape(B, T, N_BRANCH, D_MODEL)
    merged = (gates[:, :, 0] * jnp.einsum('btc,cd->btd', ua, w_branch_a_l)
              + gates[:, :, 1] * jnp.einsum('btc,cd->btd', ub, w_branch_b_l)
              + gates[:, :, 2] * jnp.einsum('btc,cd->btd', um, w_branch_m_l))
    x = x + jnp.einsum('btd,de->bte', merged, w_out_l).astype(x.dtype)
    return x, win_rows, hg_state


def setup_inputs(seed: int = 0) -> dict:
    key = jax.random.key(seed)
    ks = jax.random.split(key, 20)

    def nrm(k, shape, scale=1.0):
        return jax.random.normal(k, shape, F32) * scale

    def gain(k, shape):
        return 1.0 + 0.02 * jax.random.normal(k, shape, F32)

    win_len = [min(w, PAST_LEN) for w, _ in WIN_GROUPS]
    kv_tail = (2, A_HEADS, A_HEAD_DIM)
    return {
        'x_prompt': nrm(ks[0], (BATCH, SEQ, D_MODEL)),
        'x_sample': nrm(ks[1], (DEC_BATCH, DEC_SEQ, D_MODEL)),
        'mem_prompt': nrm(ks[2], (BATCH, MEM_LEN, D_MODEL)),
        'cache_w1_kv': nrm(ks[3], (DEPTH, DEC_BATCH, win_len[0]) + kv_tail),
        'cache_w2_kv': nrm(ks[4], (DEPTH, DEC_BATCH, win_len[1]) + kv_tail),
        'cache_w3_kv': nrm(ks[5], (DEPTH, DEC_BATCH, win_len[2]) + kv_tail),
        'cache_mem_kv': nrm(ks[6], (DEPTH, DEC_BATCH, MEM_LEN, 2, MEM_HEADS, MEM_HEAD_DIM)),
        'state_hgrn': nrm(ks[7], (DEPTH, DEC_BATCH, HG_HEADS, HG_KEY_DIM, HG_VAL_DIM), 0.5),
        'norm_in': gain(ks[8], (DEPTH, D_MODEL)),
        'w_in': nrm(ks[9], (DEPTH, D_MODEL, D_IN), D_MODEL ** -0.5),
        'lb_logits': nrm(ks[10], (DEPTH + 1, D_B), 0.5),
        'norm_hgrn': gain(ks[11], (DEPTH, D_B)),
        'norm_mem': gain(ks[12], (DEPTH, D_MODEL)),
        'w_mem_kv': nrm(ks[13], (DEPTH, D_MODEL, 2 * D_M), D_MODEL ** -0.5),
        'w_branch_a': nrm(ks[14], (DEPTH, D_A, D_MODEL), D_A ** -0.5),
        'w_branch_b': nrm(ks[15], (DEPTH, D_B, D_MODEL), D_B ** -0.5),
        'w_branch_m': nrm(ks[16], (DEPTH, D_M, D_MODEL), D_M ** -0.5),
        'w_out': nrm(ks[17], (DEPTH, D_MODEL, D_MODEL), D_MODEL ** -0.5),
        'norm_final': gain(ks[18], (D_MODEL,)),
    }


def reference(x_prompt, x_sample, mem_prompt, cache_w1_kv, cache_w2_kv, cache_w3_kv, cache_mem_kv,
              state_hgrn, norm_in, w_in, lb_logits, norm_hgrn, norm_mem, w_mem_kv, w_branch_a,
              w_branch_b, w_branch_m, w_out, norm_final):
    lb_all = jnp.cumsum(jax.nn.softmax(lb_logits.astype(F32), axis=0), axis=0)
    xp, xs = x_prompt, x_sample
    pw1, pw2, pw3, pmem, phg = [], [], [], [], []
    sw1, sw2, sw3, shg = [], [], [], []
    for l in range(DEPTH):
        mem_kv_p = memory_kv(mem_prompt, norm_mem[l], w_mem_kv[l])
        hg0 = jnp.zeros((xp.shape[0], HG_HEADS, HG_KEY_DIM, HG_VAL_DIM), F32)
        xp, rows_p, hg_p = layer_forward(xp, window_mixer_prompt, hg0, mem_kv_p, lb_all[l],
                                         norm_in[l], w_in[l], norm_hgrn[l], w_branch_a[l],
                                         w_branch_b[l], w_branch_m[l], w_out[l])
        step_fn = functools.partial(window_mixer_step, bufs=(cache_w1_kv[l], cache_w2_kv[l], cache_w3_kv[l]))
        xs, rows_s, hg_s = layer_forward(xs, step_fn, state_hgrn[l], cache_mem_kv[l], lb_all[l],
                                         norm_in[l], w_in[l], norm_hgrn[l], w_branch_a[l],
                                         w_branch_b[l], w_branch_m[l], w_out[l])
        pw1.append(rows_p[0]); pw2.append(rows_p[1]); pw3.append(rows_p[2])
        pmem.append(mem_kv_p); phg.append(hg_p)
        sw1.append(rows_s[0]); sw2.append(rows_s[1]); sw3.append(rows_s[2])
        shg.append(hg_s)
    y_prompt = rms_norm(xp, norm_final)
    y_sample = rms_norm(xs, norm_final)
    return (y_prompt, y_sample, jnp.stack(pw1), jnp.stack(pw2), jnp.stack(pw3), jnp.stack(pmem),
            jnp.stack(phg), jnp.stack(sw1), jnp.stack(sw2), jnp.stack(sw3), jnp.stack(shg))
```

```python
import contextlib
import numpy as np
import concourse.bass as bass
import concourse.mybir as mybir
from concourse.bass_utils import run_bass_kernel_spmd

F32 = mybir.dt.float32
BF16 = mybir.dt.bfloat16
AF = mybir.ActivationFunctionType
ALU = mybir.AluOpType
AX = mybir.AxisListType

NT = 2048
NS = 16
NTS = NT + NS
KC = 8
EPS = 1e-6
DILS = (1, 4, 16)
HALO = (128, 512, 2048)
ENGS = ("pe", "act", "dve", "pool", "sp")


class Buf:
    __slots__ = ("name", "last_w", "readers")

    def __init__(self, name=""):
        self.name = name
        self.last_w = None
        self.readers = []


class Slot:
    __slots__ = ("name", "sem", "count", "last", "inc")

    def __init__(self, name, inc=16):
        self.name = name
        self.sem = None
        self.count = 0
        self.last = None
        self.inc = inc


class Op:
    __slots__ = ("eng", "fn", "waits", "signal", "sigval", "slot", "slotval", "idx", "nop")


class Sched:
    def __init__(self, nc):
        self.nc = nc
        self.streams = {e: [] for e in ENGS}
        self.seen = {e: {f: -1 for f in ENGS} for e in ENGS}
        self.seen_slot = {e: {} for e in ENGS}
        self.slots = []

    def slot(self, name, inc=16):
        s = Slot(name, inc)
        self.slots.append(s)
        return s

    def _add_wait(self, o, d, kind):
        eng = o.eng
        if d is o:
            return
        if d.slot is not None:
            cur = self.seen_slot[eng].get(d.slot, 0)
            if d.slotval > cur:
                o.waits.append(d)
                self.seen_slot[eng][d.slot] = d.slotval
            return
        if d.eng == eng and eng == "pe":
            return
        if d.idx <= self.seen[eng][d.eng]:
            return
        self.seen[eng][d.eng] = d.idx
        o.waits.append(d)
        d.signal = True

    def op(self, eng, fn, reads=(), writes=(), slot=None):
        o = Op()
        o.eng, o.fn, o.waits, o.signal, o.sigval, o.slot = eng, fn, [], False, None, slot
        o.nop = False
        o.idx = len(self.streams[eng])
        if slot is not None:
            slot.count += 1
            o.slotval = slot.inc * slot.count
            slot.last = o
        for b in reads:
            if b.last_w is not None:
                self._add_wait(o, b.last_w, "raw")
        for b in writes:
            if b.last_w is not None:
                self._add_wait(o, b.last_w, "waw")
            for r in b.readers:
                self._add_wait(o, r, "war")
        for b in reads:
            b.readers.append(o)
        for b in writes:
            b.last_w = o
            b.readers = []
        self.streams[eng].append(o)
        return o

    def barrier(self):
        lasts = []
        for e in ENGS:
            for o in reversed(self.streams[e]):
                if o.slot is None and not o.nop:
                    lasts.append(o)
                    break
        dmas = [s.last for s in self.slots if s.last is not None]
        for e in ENGS:
            o = Op()
            o.eng, o.waits, o.signal, o.sigval, o.slot = e, [], False, None, None
            o.nop = True
            o.fn = lambda eng: eng.nop()
            o.idx = len(self.streams[e])
            for d in lasts + dmas:
                self._add_wait(o, d, "raw")
            self.streams[e].append(o)

    def emit(self):
        nc = self.nc
        with contextlib.ExitStack() as st:
            esem = {e: st.enter_context(nc.semaphore("sem_" + e)) for e in ("pe", "act", "dve", "pool", "sp")}
            for s in self.slots:
                if s.count > 0:
                    s.sem = st.enter_context(nc.semaphore("slot_" + s.name))
            for e in ENGS:
                c = 0
                for o in self.streams[e]:
                    if o.slot is None and o.signal:
                        c += 1
                        o.sigval = c
            block = st.enter_context(nc.Block())
            engobj = {"pe": block.tensor, "act": block.scalar, "dve": block.vector,
                      "pool": block.gpsimd, "sp": block.sync}

            def make(e):
                def body(eng):
                    for o in self.streams[e]:
                        for d in o.waits:
                            if d.slot is not None:
                                eng.wait_ge(d.slot.sem, d.slotval)
                            else:
                                eng.wait_ge(esem[d.eng], d.sigval)
                        ins = o.fn(eng)
                        if o.slot is not None:
                            ins.then_inc(o.slot.sem, o.slot.inc)
                        elif o.signal:
                            ins.then_inc(esem[e], 1)
                    if e == "sp":
                        for s in self.slots:
                            if s.count > 0:
                                eng.wait_ge(s.sem, s.inc * s.count)
                return body

            for e in ENGS:
                engobj[e](make(e))


class Arena:
    def __init__(self, nc, words):
        self.ap = nc.alloc_sbuf_tensor("arena", [128, words], F32).ap()
        self.words = words
        self.top = 0

    def alloc(self, free_shape, dtype):
        n = int(np.prod(free_shape))
        nbytes = n * (2 if dtype == BF16 else 4)
        words = (nbytes + 3) // 4
        words = (words + 7) // 8 * 8
        off = self.top
        self.top += words
        assert self.top <= self.words, ("arena overflow", self.top, self.words)
        v = self.ap[:, off:off + words]
        if dtype == BF16:
            v = v.bitcast(BF16)
        v = v[:, 0:n]
        if len(free_shape) == 2:
            v = v.rearrange("p (a b) -> p a b", b=free_shape[1])
        elif len(free_shape) == 3:
            v = v.rearrange("p (a b c) -> p a b c", b=free_shape[1], c=free_shape[2])
        return v

    def mark(self):
        return self.top

    def release(self, m):
        self.top = m


class Prog:
    def __init__(self, debug=()):
        self.debug = set(debug)
        nc = bass.Bass("TRN2", target_bir_lowering=False)
        self.nc = nc
        self.S = Sched(nc)
        self.din = {}
        self.dout = {}
        self.nslot = 0

    def inp(self, name, shape):
        self.din[name] = self.nc.dram_tensor(name, list(shape), F32, kind="ExternalInput").ap()
        return self.din[name]

    def outp(self, name, shape):
        self.dout[name] = self.nc.dram_tensor(name, list(shape), F32, kind="ExternalOutput").ap()
        return self.dout[name]

    def newslot(self, name="s"):
        self.nslot += 1
        return self.S.slot("%s%d" % (name, self.nslot))

    def dma(self, out, in_, reads, writes, slot, eng="sp", slow=False):
        if slow:
            return self.S.op(eng, lambda e: e.dma_start(out=out, in_=in_, allow_slow_non_contiguous=True),
                             reads=reads, writes=writes, slot=slot)
        return self.S.op(eng, lambda e: e.dma_start(out=out, in_=in_), reads=reads, writes=writes, slot=slot)

    def mm(self, out, lhsT, rhs, start, stop, reads, writes):
        return self.S.op("pe", lambda e: e.matmul(out, lhsT=lhsT, rhs=rhs, start=start, stop=stop),
                         reads=reads, writes=writes)

    def transpose(self, out, in_, ident, reads, writes):
        return self.S.op("pe", lambda e: e.transpose(out=out, in_=in_, identity=ident), reads=reads, writes=writes)

    def act(self, out, in_, func, reads, writes, scale=None, bias=None, accum_out=None):
        kw = {}
        if scale is not None:
            kw["scale"] = scale
        if bias is not None:
            kw["bias"] = bias
        if accum_out is not None:
            kw["accum_out"] = accum_out
        return self.S.op("act", lambda e: e.activation(out=out, in_=in_, func=func, **kw), reads=reads, writes=writes)

    def tt(self, eng, out, in0, in1, op, reads, writes):
        return self.S.op(eng, lambda e: e.tensor_tensor(out=out, in0=in0, in1=in1, op=op), reads=reads, writes=writes)

    def ts(self, eng, out, in0, s1, s2, op0, op1, reads, writes):
        if op1 is None:
            return self.S.op(eng, lambda e: e.tensor_scalar(out=out, in0=in0, scalar1=s1, scalar2=None, op0=op0),
                             reads=reads, writes=writes)
        return self.S.op(eng, lambda e: e.tensor_scalar(out=out, in0=in0, scalar1=s1, scalar2=s2, op0=op0, op1=op1),
                         reads=reads, writes=writes)

    def stt(self, out, in0, scalar, in1, op0, op1, reads, writes):
        return self.S.op("dve", lambda e: e.scalar_tensor_tensor(out=out, in0=in0, scalar=scalar, in1=in1, op0=op0, op1=op1),
                         reads=reads, writes=writes)

    def copy(self, eng, out, in_, reads, writes):
        if eng == "act":
            return self.S.op("act", lambda e: e.copy(out=out, in_=in_), reads=reads, writes=writes)
        return self.S.op(eng, lambda e: e.tensor_copy(out=out, in_=in_), reads=reads, writes=writes)

    def memset(self, eng, ap, val, writes):
        return self.S.op(eng, lambda e: e.memset(ap, val), writes=writes)


class StopBuild(Exception):
    pass


def build(debug=(), phases=("H", "A", "B", "M", "O"), stop_at=None):
    P = Prog(debug)
    try:
        _build(P, phases, stop_at)
    except StopBuild:
        pass
    P.S.emit()
    return P


def _build(P, phases, stop_at):
    def chk(tag):
        if tag == stop_at:
            raise StopBuild()

    nc, S = P.nc, P.S
    xo = P.inp("xo", [NT, 1024]); xh = P.inp("xh", [NT, 1024]); xs = P.inp("xs", [NS, 1024])
    mem = P.inp("mem", [256, 1024])
    cw = [P.inp("cw1", [NS, 128, 512]), P.inp("cw2", [NS, 512, 512]), P.inp("cw3", [NS, 2048, 512])]
    cmem = P.inp("cmem", [NS, 256, 512]); sh = P.inp("sh", [NS, 4, 128, 128])
    w_in = P.inp("w_in", [1024, 8192]); w_mem = P.inp("w_mem", [1024, 512])
    w_a = P.inp("w_a", [256, 1024]); w_b = P.inp("w_b", [512, 1024]); w_m = P.inp("w_m", [256, 1024])
    w_out = P.inp("w_out", [1024, 1024])
    gin = P.inp("gin", [128, 1024]); gmem = P.inp("gmem", [128, 1024]); gfin = P.inp("gfin", [128, 1024])
    lbl = P.inp("lbl", [128, 8]); ghg = P.inp("ghg", [128, 4])
    c_ident = P.inp("c_ident", [128, 128]); c_band = P.inp("c_band", [128, 256]); c_bandh = P.inp("c_bandh", [128, 256])
    c_caus = P.inp("c_caus", [128, 64]); c_blk = P.inp("c_blk", [128, 128]); c_sel = P.inp("c_sel", [16, 2048])
    c_flags = P.inp("c_flags", [128, 4])

    y_o = P.outp("y", [NT, 1024]); ys_o = P.outp("ys", [NS, 1024])
    kv_o = [P.outp("kv1", [128, 512]), P.outp("kv2", [512, 512]), P.outp("kv3", [2048, 512])]
    memkv_o = P.outp("memkv", [256, 512]); hg_o = P.outp("hg", [4, 128, 128])
    skv_o = [P.outp("skv1", [NS, 512]), P.outp("skv2", [NS, 512]), P.outp("skv3", [NS, 512])]
    shg_o = P.outp("shg", [NS, 4, 128, 128])

    w_in_v = w_in.rearrange("(k p) c -> p k c", p=128)

    A = Arena(nc, 53200)
    psb = [nc.alloc_psum_tensor("psb%d" % i, [128, 512], F32).ap() for i in range(8)]
    psb_b = [Buf("ps%d" % i) for i in range(8)]
    P.bank = 0

    def next_bank(lo=0, hi=8):
        b = lo + (P.bank % (hi - lo))
        P.bank += 1
        return b

    def dbg(name, ap, shape, reads):
        if name not in P.debug:
            return
        o = P.outp("dbg_" + name, shape)
        P.dma(o, ap, reads=reads, writes=[Buf()], slot=P.newslot("dbg"), eng="pool")

    hT = A.alloc([KC, NTS], BF16); hT_b = [Buf("hT%d" % i) for i in range(5)]
    hmT = A.alloc([KC, 256], BF16); hmT_b = Buf("hmT")
    ident = A.alloc([128], BF16); band = A.alloc([256], BF16); bandh = A.alloc([256], BF16)
    caus = A.alloc([64], BF16); blk = A.alloc([128], BF16); ones = A.alloc([128], BF16)
    sel = A.alloc([16, 128], BF16); flags = A.alloc([4], F32)
    const_b = Buf("consts")
    for dst, src in ((ident, c_ident), (band, c_band), (bandh, c_bandh), (caus, c_caus), (blk, c_blk)):
        P.dma(dst, src, reads=[], writes=[const_b], slot=P.newslot("c"), eng="pool")
    P.dma(sel[0:16].rearrange("p a b -> p (a b)"), c_sel, reads=[], writes=[const_b], slot=P.newslot("c"), eng="pool")
    P.dma(flags, c_flags, reads=[], writes=[const_b], slot=P.newslot("c"))
    P.memset("pool", ones, 1.0, writes=[const_b])
    ssq = A.alloc([64], F32); lnv = A.alloc([64], F32); rstd = A.alloc([64], F32)
    st_b = [Buf("st%d" % i) for i in range(64)]
    P.tilectr = 0
    uaT = A.alloc([2, NTS], BF16); uaT_b = [Buf("uaT0"), Buf("uaT1")]

    def norm_transpose(jobs, xst, xst_b, xst_s, xn, xn_b, junk, pst, pst_b):
        tiles = []
        for (src, ntok, gain_ap, gain_b, dst, dst_buf_fn, col0) in jobs:
            for t in range((ntok + 127) // 128):
                tiles.append((src, min(128, ntok - t * 128), gain_ap, gain_b, dst, dst_buf_fn, col0, t))
        js = []

        def stage_a(i):
            src, part, gain_ap, gain_b, dst, dst_buf_fn, col0, t = tiles[i]
            j = P.tilectr
            P.tilectr += 1
            js.append(j)
            k = i % 4
            P.dma(xst[k][0:part], src[t * 128:t * 128 + part, :], reads=[], writes=[xst_b[k]], slot=xst_s[k])
            P.act(junk[0:part], xst[k][0:part], AF.Square, reads=[xst_b[k]], writes=[st_b[j]],
                  accum_out=ssq[0:part, j:j + 1])
            P.act(lnv[0:part, j:j + 1], ssq[0:part, j:j + 1], AF.Ln, reads=[st_b[j]], writes=[st_b[j]],
                  scale=1.0 / 1024, bias=EPS)
            P.act(rstd[0:part, j:j + 1], lnv[0:part, j:j + 1], AF.Exp, reads=[st_b[j]], writes=[st_b[j]], scale=-0.5)

        def stage_b(i):
            src, part, gain_ap, gain_b, dst, dst_buf_fn, col0, t = tiles[i]
            j = js[i]
            k = i % 4
            P.stt(xn[k][0:part], xst[k][0:part], rstd[0:part, j:j + 1], gain_ap[0:part], ALU.mult, ALU.mult,
                  reads=[xst_b[k], st_b[j], gain_b], writes=[xn_b[k]])
            for kc in range(KC):
                P.transpose(pst[i % 2][:, kc * 128:kc * 128 + part], xn[k][0:part, kc * 128:(kc + 1) * 128],
                            ident[0:part, 0:part], reads=[xn_b[k], const_b], writes=[pst_b[i % 2]])

        def stage_c(i):
            src, part, gain_ap, gain_b, dst, dst_buf_fn, col0, t = tiles[i]
            P.copy("act" if i % 2 == 0 else "dve", dst[:, :, col0 + t * 128:col0 + t * 128 + part],
                   pst[i % 2].rearrange("p (k t) -> p k t", t=128)[:, :, 0:part], reads=[pst_b[i % 2]], writes=[dst_buf_fn(t)])

        n = len(tiles)
        for i in range(n + 2):
            if i < n:
                stage_a(i)
            if 0 <= i - 1 < n:
                stage_b(i - 1)
            if 0 <= i - 2 < n:
                stage_c(i - 2)

    mA = A.mark()
    hTh = A.alloc([KC, NT], BF16); hTh_b = [Buf("hTh%d" % i) for i in range(4)]
    Wa = A.alloc([KC, 10, 128], BF16); Wa_b = [Buf("Wa%d" % i) for i in range(10)]
    Wa_s = [P.newslot("wa") for _ in range(10)]

    def load_wa(hp, only=None):
        wcols = [0] * 10
        for g in range(3):
            wcols[g] = g * 256 + hp * 128
            wcols[3 + 2 * g] = 768 + g * 256 + hp * 128
            wcols[4 + 2 * g] = 1536 + g * 256 + hp * 128
        wcols[9] = 2304 + hp * 128
        for c in range(10):
            if only is not None and c != only:
                continue
            P.dma(Wa[:, :, c, :], w_in_v[:, :, wcols[c]:wcols[c] + 128], reads=[], writes=[Wa_b[c]],
                  slot=Wa_s[c], eng="pool")

    if "A" in phases:
        load_wa(0)
    m_h = A.mark()
    xst = [A.alloc([1024], F32) for _ in range(4)]; xst_b = [Buf() for _ in range(4)]
    xst_s = [P.newslot("x") for _ in range(4)]
    xn = [A.alloc([1024], BF16) for _ in range(4)]; xn_b = [Buf() for _ in range(4)]
    junk = A.alloc([1024], BF16)
    gbc = A.alloc([1024], F32); gbc_b = Buf("gbc")
    gbc2 = A.alloc([1024], F32); gbc2_b = Buf("gbc2")
    pst = [psb[6].bitcast(BF16), psb[7].bitcast(BF16)]; pst_b = [psb_b[6], psb_b[7]]
    P.dma(gbc, gin, reads=[], writes=[gbc_b], slot=P.newslot("g"))
    P.dma(gbc2, gmem, reads=[], writes=[gbc2_b], slot=P.newslot("g"))
    hargs = (xst, xst_b, xst_s, xn, xn_b, junk, pst, pst_b)
    norm_transpose([(xo, NT, gbc, gbc_b, hT, lambda t: hT_b[t // 4], 0),
                    (xs, NS, gbc, gbc_b, hT, lambda t: hT_b[4], NT),
                    (xh, NT, gbc, gbc_b, hTh, lambda t: hTh_b[t // 4], 0),
                    (mem, 256, gbc2, gbc2_b, hmT, lambda t: hmT_b, 0)], *hargs)
    dbg("hT", hT[:, 0, :], [128, NTS], hT_b)
    dbg("hTh", hTh[:, 7, :], [128, NT], hTh_b)
    S.barrier()
    A.release(m_h)

    if "A" in phases:
        QT = A.alloc([3, NTS], BF16); QT_b = [[Buf() for _ in range(5)] for _ in range(3)]
        KTlen = [HALO[g] + NTS for g in range(3)]
        KT = [A.alloc([KTlen[g]], BF16) for g in range(3)]
        KT_b = [[Buf() for _ in range(6)] for _ in range(3)]
        nblk = [16 // d for d in DILS]
        nvt = [DILS[g] + 16 for g in range(3)]
        V = [A.alloc([nvt[g], 2, 64], BF16) for g in range(3)]
        V_b = [[Buf() for _ in range(nvt[g])] for g in range(3)]
        gaT = A.alloc([NTS], BF16); gaT_b = [Buf() for _ in range(5)]
        acc_o = A.alloc([NTS], F32); acc_l = A.alloc([NTS], F32)
        acc_b = [Buf("acc%d" % i) for i in range(5)]
        pT = [A.alloc([256], BF16) for _ in range(6)]; pT_b = [Buf() for _ in range(6)]
        pTm = [A.alloc([256], BF16) for _ in range(6)]; pTm_b = [Buf() for _ in range(6)]
        kst = [A.alloc([256], F32) for _ in range(2)]; kst_b = [Buf(), Buf()]; kst_s = [P.newslot("ks"), P.newslot("ks")]
        Cc = A.alloc([NS * 3 * 2 * 128], BF16).rearrange("p (s g a c) -> p s g a c", s=NS, g=3, a=2)
        Cc_b2 = [[Buf(), Buf()] for _ in range(3)]; Cc_b = [b for bb in Cc_b2 for b in bb]
        Cc_s = [[P.newslot("cc"), P.newslot("cc")] for _ in range(3)]
        qs_tm = A.alloc([384], BF16); qs_b = Buf()
        prod = A.alloc([384], F32); prod_b = Buf()
        sc = A.alloc([NS * 6], F32); sc_b = Buf()
        pTs = A.alloc([NS * 6], BF16); pTs_b = Buf()
        qk3 = A.alloc([3, NS], BF16); qk_b = Buf()
        es = A.alloc([3, NS], F32); es_b = Buf()
        vTs = A.alloc([3, NS], F32); vTs_b = Buf()
        ev = A.alloc([3, NS], F32); ev_b = Buf()
        red = A.alloc([2, NS], F32); red_b = Buf()
        print("arena top phase A (KB):", A.top * 4 / 1024)

        def tile_cols(g, r, nb, base):
            d = DILS[g]
            s0 = base + r + d * 128 * nb
            return slice(s0, s0 + d * 127 + 1, d)

        for hp in range(2):
            for g in range(3):
                src = cw[g][:, 0:HALO[g]:DILS[g], :].rearrange("s r (a c) -> r s a c", a=2)[:, :, :, hp * 128:(hp + 1) * 128]
                for a in range(2):
                    P.dma(Cc[:, :, g, a, :], src[:, :, a, :], reads=[], writes=[Cc_b2[g][a]], slot=Cc_s[g][a], eng="pool")

            def fm_group(c, rhs_fn, n, rd, evac, lo=0, hi=8):
                bank = next_bank(lo, hi)
                for kc in range(KC):
                    P.mm(psb[bank][:, 0:n], Wa[:, kc, c, :], rhs_fn(kc), kc == 0, kc == KC - 1,
                         reads=[Wa_b[c]] + rd, writes=[psb_b[bank]])
                evac(psb[bank][:, 0:n], psb_b[bank])

            def qk_thunks(lo, hi):
                th = []
                for g in range(3):
                    for tb in range(4):
                        c0, n = tb * 512, 512
                        th.append(lambda g=g, c0=c0, n=n, tb=tb: fm_group(
                            g, lambda kc: hT[:, kc, c0:c0 + n], n, [hT_b[tb]],
                            lambda ps, pb: P.ts("dve", QT[:, g, c0:c0 + n], ps, 0.125, None, ALU.mult, None, reads=[pb], writes=[QT_b[g][tb]]),
                            lo, hi))
                        th.append(lambda g=g, c0=c0, n=n, tb=tb: fm_group(
                            3 + 2 * g, lambda kc: hT[:, kc, c0:c0 + n], n, [hT_b[tb]],
                            lambda ps, pb: P.copy("dve", KT[g][:, HALO[g] + c0:HALO[g] + c0 + n], ps, reads=[pb], writes=[KT_b[g][1 + tb]]),
                            lo, hi))
                return th

            if hp == 0:
                for th_ in qk_thunks(0, 8):
                    th_()
            for g in range(3):
                for tb in range(4, 5):
                    c0, n = (tb * 512, 512) if tb < 4 else (NT, NS)
                    fm_group(g, lambda kc, c0=c0, n=n: hT[:, kc, c0:c0 + n], n, [hT_b[tb]],
                             lambda ps, pb, g=g, c0=c0, n=n, tb=tb: P.ts("dve", QT[:, g, c0:c0 + n], ps, 0.125, None, ALU.mult, None,
                                                                reads=[pb], writes=[QT_b[g][tb]]))
                    fm_group(3 + 2 * g, lambda kc, c0=c0, n=n: hT[:, kc, c0:c0 + n], n, [hT_b[tb]],
                             lambda ps, pb, g=g, c0=c0, n=n, tb=tb: P.copy("dve", KT[g][:, HALO[g] + c0:HALO[g] + c0 + n], ps,
                                                                  reads=[pb], writes=[KT_b[g][1 + tb]]))
                hl = HALO[g]
                for c0 in range(NT - hl, NT, 512):
                    n = min(512, NT - c0)
                    fm_group(3 + 2 * g, lambda kc, c0=c0, n=n: hTh[:, kc, c0:c0 + n], n, hTh_b,
                             lambda ps, pb, g=g, c0=c0, n=n, hl=hl: P.copy("dve", KT[g][:, c0 - (NT - hl):c0 - (NT - hl) + n], ps,
                                                                    reads=[pb], writes=[KT_b[g][0]]))
            for tb in range(5):
                c0, n = (tb * 512, 512) if tb < 4 else (NT, NS)
                fm_group(9, lambda kc, c0=c0, n=n: hT[:, kc, c0:c0 + n], n, [hT_b[tb]],
                         lambda ps, pb, c0=c0, n=n, tb=tb: P.act(gaT[:, c0:c0 + n], ps, AF.Silu, reads=[pb], writes=[gaT_b[tb]]))
            bank = next_bank()
            for g in range(3):
                for kc in range(KC):
                    P.mm(psb[bank][:, g * NS:(g + 1) * NS], Wa[:, kc, 4 + 2 * g, :], hT[:, kc, NT:NTS], kc == 0, kc == KC - 1,
                         reads=[Wa_b[4 + 2 * g], hT_b[4]], writes=[psb_b[bank]])
            P.copy("act", vTs.rearrange("p g s -> p (g s)"), psb[bank][:, 0:3 * NS], reads=[psb_b[bank]], writes=[vTs_b])
            bank = next_bank()
            for kc in range(KC):
                P.mm(psb[bank][0:NS, 0:384], hT[:, kc, NT:NTS], Wa[:, kc, 0:3, :].rearrange("p a b -> p (a b)"),
                     kc == 0, kc == KC - 1, reads=[Wa_b[0], Wa_b[1], Wa_b[2], hT_b[4]], writes=[psb_b[bank]])
            P.act(qs_tm[0:NS], psb[bank][0:NS, 0:384], AF.Copy, reads=[psb_b[bank]], writes=[qs_b], scale=0.125)
            kctr = 0
            for g in range(3):
                d = DILS[g]
                plain, special = [], []
                for r in range(d):
                    plain.append((r, hTh, hTh_b, tile_cols(g, r, 0, NT - HALO[g])))
                for r in range(d):
                    for nb in range(nblk[g]):
                        ti_ = d + r * nblk[g] + nb
                        if d * 128 * nb == NT - HALO[g]:
                            special.append((ti_, r, tile_cols(g, r, nb, 0)))
                        else:
                            plain.append((ti_, hT, hT_b[0:4], tile_cols(g, r, nb, 0)))
                for t0 in range(0, len(plain), 4):
                    bank = next_bank()
                    grp = plain[t0:t0 + 4]
                    for ti, (vt, srcT, srcb, cs) in enumerate(grp):
                        for kc in range(KC):
                            P.mm(psb[bank][:, ti * 128:(ti + 1) * 128], srcT[:, kc, cs], Wa[:, kc, 4 + 2 * g, :],
                                 kc == 0, kc == KC - 1, reads=[Wa_b[4 + 2 * g]] + list(srcb), writes=[psb_b[bank]])
                    contiguous = all(grp[i + 1][0] == grp[i][0] + 1 for i in range(len(grp) - 1))
                    if contiguous:
                        v0 = grp[0][0]
                        P.copy("act", V[g][:, v0:v0 + len(grp), :, :],
                               psb[bank][:, 0:len(grp) * 128].rearrange("p (t h d) -> p t h d", h=2, d=64),
                               reads=[psb_b[bank]], writes=V_b[g][v0:v0 + len(grp)])
                    else:
                        for ti, (vt, srcT, srcb, cs) in enumerate(grp):
                            P.copy("act", V[g][:, vt, :, :], psb[bank][:, ti * 128:(ti + 1) * 128].rearrange("p (h d) -> p h d", d=64),
                                   reads=[psb_b[bank]], writes=[V_b[g][vt]])
                for (vt, r, cs) in special:
                    bank = next_bank()
                    for kc in range(KC):
                        P.mm(psb[bank][:, 0:256], hT[:, kc, cs], Wa[:, kc, 3 + 2 * g:5 + 2 * g, :].rearrange("p a b -> p (a b)"),
                             kc == 0, kc == KC - 1, reads=[Wa_b[3 + 2 * g], Wa_b[4 + 2 * g]] + hT_b[0:4], writes=[psb_b[bank]])
                    k = kctr % 2
                    kctr += 1
                    P.copy("act" if kctr % 2 else "dve", kst[k], psb[bank][:, 0:256], reads=[psb_b[bank]], writes=[kst_b[k]])
                    P.copy("pool", V[g][:, vt, :, :], kst[k][:, 128:256].rearrange("p (h d) -> p h d", d=64), reads=[kst_b[k]], writes=[V_b[g][vt]])
                    dst = kv_o[g][r:r + d * 127 + 1:d, :].rearrange("t (a c) -> t a c", a=2)[:, :, hp * 128:(hp + 1) * 128]
                    P.dma(dst, kst[k].rearrange("p (a c) -> p a c", a=2), reads=[kst_b[k]], writes=[Buf()], slot=kst_s[k])
                bank = next_bank()
                for kc in range(KC):
                    P.mm(psb[bank][0:NS, 0:256], hT[:, kc, NT:NTS],
                         Wa[:, kc, 3 + 2 * g:5 + 2 * g, :].rearrange("p a b -> p (a b)"), kc == 0, kc == KC - 1,
                         reads=[Wa_b[3 + 2 * g], Wa_b[4 + 2 * g], hT_b[4]], writes=[psb_b[bank]])
                k = kctr % 2
                kctr += 1
                P.copy("act" if kctr % 2 else "dve", kst[k][0:NS], psb[bank][0:NS, 0:256], reads=[psb_b[bank]], writes=[kst_b[k]])
                dst = skv_o[g][:, :].rearrange("t (a c) -> t a c", a=2)[:, :, hp * 128:(hp + 1) * 128]
                P.dma(dst, kst[k][0:NS].rearrange("p (a c) -> p a c", a=2), reads=[kst_b[k]], writes=[Buf()], slot=kst_s[k])
            if hp == 0:
                dbg("QT", QT[:, 0, :], [128, NTS], [b for bb in QT_b for b in bb])
                dbg("KT3", KT[2], [128, KTlen[2]], KT_b[2])
                dbg("V3", V[2].rearrange("p t h d -> p (t h d)"), [128, nvt[2] * 128], V_b[2])
            ps_s = [(psb[i][:, 0:256], psb_b[i]) for i in range(6)]
            ps_o = [(psb[6 + i][:, 0:128], psb_b[6 + i]) for i in range(2)]
            ps_l = [(psb[6 + i][:, 128:256], psb_b[6 + i]) for i in range(2)]
            items = []
            for g in range(3):
                d = DILS[g]
                for r in range(d):
                    for nb in range(nblk[g]):
                        qcols = tile_cols(g, r, nb, 0)
                        tbs = [nb // 4] if g == 0 else ([nb] if g == 1 else [0, 1, 2, 3])
                        cur_t = d + r * nblk[g] + nb
                        cur_c = tile_cols(g, r, nb, HALO[g])
                        if nb == 0:
                            prev_t, prev_c, mask, kprev_b = r, tile_cols(g, r, 0, 0), bandh, [KT_b[g][0]]
                        else:
                            prev_t, prev_c, mask = cur_t - 1, tile_cols(g, r, nb - 1, HALO[g]), band
                            kprev_b = KT_b[g][1:5]
                        for hh in range(2):
                            items.append((g, qcols, tbs, cur_t, cur_c, prev_t, prev_c, mask, kprev_b, hh))
            LOOK = 4
            NPT = 6

            def front(j):
                g, qcols, tbs, cur_t, cur_c, prev_t, prev_c, mask, kprev_b, hh = items[j]
                hr = slice(64 * hh, 64 * hh + 64)
                ps, psbuf = ps_s[j % 6]
                pti = j % NPT
                P.mm(ps[:, 0:128], KT[g][hr, prev_c], QT[hr, g, qcols], True, True, reads=kprev_b + QT_b[g][0:4], writes=[psbuf])
                P.mm(ps[:, 128:256], KT[g][hr, cur_c], QT[hr, g, qcols], True, True, reads=KT_b[g][1:5] + QT_b[g][0:4], writes=[psbuf])
                P.act(pT[pti], ps, AF.Exp, reads=[psbuf], writes=[pT_b[pti]])
                P.tt("pool" if j % 2 == 0 else "dve", pTm[pti], pT[pti], mask, ALU.mult, reads=[pT_b[pti], const_b], writes=[pTm_b[pti]])

            def back(j):
                g, qcols, tbs, cur_t, cur_c, prev_t, prev_c, mask, kprev_b, hh = items[j]
                hr = slice(64 * hh, 64 * hh + 64)
                pti = j % NPT
                oi = (j // 2) % 2
                for X, vt in ((0, prev_t), (1, cur_t)):
                    P.mm(ps_o[oi][0][hr, :], V[g][:, vt, hh, :], pTm[pti][:, X * 128:(X + 1) * 128], X == 0, X == 1,
                         reads=[V_b[g][vt], pTm_b[pti]], writes=[ps_o[oi][1]])
                for X in (0, 1):
                    P.mm(ps_l[oi][0][hr, :], ones[:, 0:64], pTm[pti][:, X * 128:(X + 1) * 128], X == 0, X == 1,
                         reads=[const_b, pTm_b[pti]], writes=[ps_l[oi][1]])
                if hh == 1:
                    ab = [acc_b[t] for t in tbs]
                    if g == 0:
                        P.copy("dve", acc_o[:, qcols], ps_o[oi][0], reads=[ps_o[oi][1]], writes=ab)
                        P.copy("dve", acc_l[:, qcols], ps_l[oi][0], reads=[ps_l[oi][1]], writes=ab)
                    else:
                        P.tt("dve", acc_o[:, qcols], acc_o[:, qcols], ps_o[oi][0], ALU.add, reads=[ps_o[oi][1]] + ab, writes=ab)
                        P.tt("dve", acc_l[:, qcols], acc_l[:, qcols], ps_l[oi][0], ALU.add, reads=[ps_l[oi][1]] + ab, writes=ab)

            for j in range(min(LOOK, len(items))):
                front(j)
            for j in range(len(items)):
                if j + LOOK < len(items):
                    front(j + LOOK)
                back(j)
                if hp == 0 and j >= 8 and (j - 8) % 4 == 0 and (j - 8) // 4 < 10:
                    load_wa(1, only=(j - 8) // 4)
            early = qk_thunks(0, 4) if hp == 0 else []
            for s in range(NS):
                if early:
                    early.pop(0)()
                P.mm(psb[6][:, 0:384], sel[0:NS, s, :], qs_tm[0:NS], True, True, reads=[const_b, qs_b], writes=[psb_b[6]])
                P.tt("dve", prod.rearrange("p (g c) -> p g c", g=3), Cc[:, s, :, 0, :],
                     psb[6][:, 0:384].rearrange("p (g c) -> p g c", g=3), ALU.mult, reads=Cc_b + [psb_b[6]], writes=[prod_b])
                S.op("dve", lambda e, s=s: e.tensor_reduce(out=sc[:, s * 6:(s + 1) * 6],
                                                        in_=prod.rearrange("p (g h d) -> p (g h) d", g=3, h=2),
                                                        axis=AX.X, op=ALU.add), reads=[prod_b], writes=[sc_b])
            P.act(pTs, sc, AF.Exp, reads=[sc_b], writes=[pTs_b])
            while early:
                early.pop(0)()
            for s in range(NS):
                for hh in range(2):
                    hr = slice(64 * hh, 64 * hh + 64)
                    for g in range(3):
                        rhs = pTs[:, s * 6 + g * 2 + hh:s * 6 + g * 2 + hh + 1]
                        P.mm(psb[4][hr, s:s + 1], Cc[:, s, g, 1, hh * 64:(hh + 1) * 64], rhs, g == 0, g == 2,
                             reads=Cc_b + [pTs_b], writes=[psb_b[4]])
                    for g in range(3):
                        rhs = pTs[:, s * 6 + g * 2 + hh:s * 6 + g * 2 + hh + 1]
                        P.mm(psb[5][hr, s:s + 1], ones[:, 0:64], rhs, g == 0, g == 2,
                             reads=[const_b, pTs_b], writes=[psb_b[5]])
            for g in range(3):
                P.tt("dve", qk3[:, g, :], QT[:, g, NT:NTS], KT[g][:, HALO[g] + NT:HALO[g] + NTS], ALU.mult,
                     reads=[QT_b[g][4], KT_b[g][5]], writes=[qk_b])
            P.mm(psb[7][:, 0:3 * NS], blk, qk3.rearrange("p g s -> p (g s)"), True, True, reads=[const_b, qk_b], writes=[psb_b[7]])
            P.act(es.rearrange("p g s -> p (g s)"), psb[7][:, 0:3 * NS], AF.Exp, reads=[psb_b[7]], writes=[es_b])
            P.tt("dve", ev, es, vTs, ALU.mult, reads=[es_b, vTs_b], writes=[ev_b])
            S.op("dve", lambda e: e.tensor_reduce(out=red[:, 0, :], in_=ev.rearrange("p g s -> p s g"), axis=AX.X, op=ALU.add),
                 reads=[ev_b], writes=[red_b])
            S.op("dve", lambda e: e.tensor_reduce(out=red[:, 1, :], in_=es.rearrange("p g s -> p s g"), axis=AX.X, op=ALU.add),
                 reads=[es_b], writes=[red_b])
            P.tt("dve", acc_o[:, NT:NTS], red[:, 0, :], psb[4][:, 0:NS], ALU.add, reads=[red_b, psb_b[4]], writes=[acc_b[4]])
            P.tt("dve", acc_l[:, NT:NTS], red[:, 1, :], psb[5][:, 0:NS], ALU.add, reads=[red_b, psb_b[5]], writes=[acc_b[4]])
            if hp == 0:
                dbg("acc_o", acc_o, [128, NTS], acc_b)
                dbg("acc_l", acc_l, [128, NTS], acc_b)
            P.act(acc_l, acc_l, AF.Ln, reads=acc_b, writes=acc_b)
            P.act(acc_l, acc_l, AF.Exp, reads=acc_b, writes=acc_b, scale=-1.0)
            P.tt("dve", acc_o, acc_o, acc_l, ALU.mult, reads=acc_b, writes=acc_b)
            P.tt("dve", uaT[:, hp, :], acc_o, gaT, ALU.mult, reads=acc_b + gaT_b, writes=[uaT_b[hp]])
        S.barrier()
        dbg("uaT", uaT.rearrange("p a t -> p (a t)"), [128, 2 * NTS], uaT_b)
    A.release(mA)

    def phase_M(hook=None):
        mM = A.mark()
        Wmk = A.alloc([KC, 4, 128], BF16); Wmk_b = Buf(); Wmk_s = P.newslot("wm")
        Wqm = A.alloc([KC, 4, 128], BF16); Wqm_bs = [Buf() for _ in range(4)]; Wqm_ss = [P.newslot("wq") for _ in range(4)]
        KmT = A.alloc([2, 256], BF16); KmT_b = Buf()
        Vmm = A.alloc([2, 4, 64], BF16); Vmm_b = Buf()
        mst = [A.alloc([512], F32) for _ in range(2)]; mst_b = [Buf(), Buf()]; mst_s = [P.newslot("ms"), P.newslot("ms")]
        QmT = A.alloc([2, NTS], BF16); QmT_b = [Buf() for _ in range(5)]
        gmT = A.alloc([2, NTS], BF16); gmT_b = [Buf() for _ in range(5)]
        pTq = [A.alloc([512], BF16) for _ in range(4)]; pTq_b = [Buf() for _ in range(4)]
        lnl = [A.alloc([512], F32) for _ in range(2)]; lnl_b = [Buf(), Buf()]
        tO = [A.alloc([512], F32) for _ in range(2)]; tO_b = [Buf(), Buf()]
        Cm = A.alloc([NS * 2 * 2 * 128], BF16).rearrange("p (s m a c) -> p s m a c", s=NS, m=2, a=2)
        Cm_b4 = [[Buf(), Buf()] for _ in range(2)]; Cm_bl = [b for bb in Cm_b4 for b in bb]
        Cm_s4 = [[P.newslot("cm"), P.newslot("cm")] for _ in range(2)]
        qsm = A.alloc([128], BF16); qsm_b = Buf()
        prodm = A.alloc([2, 128], F32); prodm_b = Buf()
        scm = A.alloc([NS * 4], F32); scm_b = Buf()
        pTsm = A.alloc([NS * 4], BF16); pTsm_b = Buf()
        P.dma(Wmk, w_mem.rearrange("(k p) (c n) -> p k c n", p=128, n=128), reads=[], writes=[Wmk_b], slot=Wmk_s, eng="pool")
        for c in range(4):
            col = (4608 if c < 2 else 4864) + (c % 2) * 128
            P.dma(Wqm[:, :, c, :], w_in_v[:, :, col:col + 128], reads=[], writes=[Wqm_bs[c]], slot=Wqm_ss[c], eng="pool")
        if hook is not None:
            hook()
        chk('m_w')
        for mt in range(2):
            bank = next_bank()
            for kc in range(KC):
                P.mm(psb[bank], hmT[:, kc, mt * 128:(mt + 1) * 128], Wmk[:, kc, :, :].rearrange("p c n -> p (c n)"), kc == 0, kc == KC - 1,
                     reads=[hmT_b, Wmk_b], writes=[psb_b[bank]])
            P.copy("act", mst[mt], psb[bank], reads=[psb_b[bank]], writes=[mst_b[mt]])
            P.copy("pool", Vmm[:, mt, :, :], mst[mt][:, 256:512].rearrange("p (h d) -> p h d", d=64), reads=[mst_b[mt]], writes=[Vmm_b])
            P.dma(memkv_o[mt * 128:(mt + 1) * 128, :], mst[mt], reads=[mst_b[mt]], writes=[Buf()], slot=mst_s[mt])
        chk('m_kv')
        for c in range(2):
            bank = next_bank()
            for kc in range(KC):
                P.mm(psb[bank][:, 0:256], Wmk[:, kc, c, :], hmT[:, kc, :], kc == 0, kc == KC - 1, reads=[hmT_b, Wmk_b], writes=[psb_b[bank]])
            P.copy("dve", KmT[:, c, :], psb[bank][:, 0:256], reads=[psb_b[bank]], writes=[KmT_b])
        for c in range(4):
            for tb in range(5):
                c0, n = (tb * 512, 512) if tb < 4 else (NT, NS)
                bank = next_bank()
                for kc in range(KC):
                    P.mm(psb[bank][:, 0:n], Wqm[:, kc, c, :], hT[:, kc, c0:c0 + n], kc == 0, kc == KC - 1,
                         reads=[Wqm_bs[c], hT_b[tb]], writes=[psb_b[bank]])
                if c < 2:
                    P.ts("dve", QmT[:, c, c0:c0 + n], psb[bank][:, 0:n], 0.125, None, ALU.mult, None, reads=[psb_b[bank]], writes=[QmT_b[tb]])
                else:
                    P.act(gmT[:, c - 2, c0:c0 + n], psb[bank][:, 0:n], AF.Silu, reads=[psb_b[bank]], writes=[gmT_b[tb]])
        chk('m_proj')
        mitems = [(hp, tb, hh) for hp in range(2) for tb in range(4) for hh in range(2)]

        def m_front(i):
            hp, tb, hh = mitems[i]
            cs = slice(tb * 512, (tb + 1) * 512)
            hr = slice(64 * hh, 64 * hh + 64)
            for mt in range(2):
                sb_ = (i % 2) * 2 + mt
                P.mm(psb[sb_], KmT[hr, hp, mt * 128:(mt + 1) * 128], QmT[hr, hp, cs], True, True,
                     reads=[KmT_b, QmT_b[tb]], writes=[psb_b[sb_]])
                P.act(pTq[sb_], psb[sb_], AF.Exp, reads=[psb_b[sb_]], writes=[pTq_b[sb_]])

        def m_back(i):
            hp, tb, hh = mitems[i]
            cs = slice(tb * 512, (tb + 1) * 512)
            hr = slice(64 * hh, 64 * hh + 64)
            h = 2 * hp + hh
            u = i // 2
            bo, bl = 4 + (u % 2) * 2, 5 + (u % 2) * 2
            for mt in range(2):
                sb_ = (i % 2) * 2 + mt
                P.mm(psb[bo][hr, :], Vmm[:, mt, h, :], pTq[sb_], mt == 0, mt == 1, reads=[Vmm_b, pTq_b[sb_]], writes=[psb_b[bo]])
            for mt in range(2):
                sb_ = (i % 2) * 2 + mt
                P.mm(psb[bl][hr, :], ones[:, 0:64], pTq[sb_], mt == 0, mt == 1, reads=[const_b, pTq_b[sb_]], writes=[psb_b[bl]])
            if hh == 1:
                k = u % 2
                P.act(lnl[k], psb[bl], AF.Ln, reads=[psb_b[bl]], writes=[lnl_b[k]])
                P.act(lnl[k], lnl[k], AF.Exp, reads=[lnl_b[k]], writes=[lnl_b[k]], scale=-1.0)
                P.tt("dve", tO[k], lnl[k], psb[bo], ALU.mult, reads=[lnl_b[k], psb_b[bo]], writes=[tO_b[k]])
                P.tt("dve", umT[:, hp, cs], tO[k], gmT[:, hp, cs], ALU.mult, reads=[tO_b[k], gmT_b[tb]], writes=[umT_b[hp]])

        m_front(0)
        for i in range(len(mitems)):
            if i + 1 < len(mitems):
                m_front(i + 1)
            m_back(i)
        for hp in range(2):
            chk('m_prompt')
            for mt in range(2):
                src = cmem[:, mt * 128:(mt + 1) * 128, :].rearrange("s r (a c) -> r s a c", a=2)[:, :, :, hp * 128:(hp + 1) * 128]
                for a in range(2):
                    P.dma(Cm[:, :, mt, a, :], src[:, :, a, :], reads=[], writes=[Cm_b4[mt][a]], slot=Cm_s4[mt][a], eng="pool")
            bank = next_bank()
            for kc in range(KC):
                P.mm(psb[bank][0:NS, 0:128], hT[:, kc, NT:NTS], Wqm[:, kc, hp, :], kc == 0, kc == KC - 1,
                     reads=[Wqm_bs[hp], hT_b[4]], writes=[psb_b[bank]])
            P.act(qsm[0:NS], psb[bank][0:NS, 0:128], AF.Copy, reads=[psb_b[bank]], writes=[qsm_b], scale=0.125)
            for s in range(NS):
                P.mm(psb[6][:, 0:128], sel[0:NS, s, :], qsm[0:NS], True, True, reads=[const_b, qsm_b], writes=[psb_b[6]])
                P.tt("dve", prodm, Cm[:, s, :, 0, :], psb[6][:, 0:128].rearrange("p (o c) -> p o c", o=1).to_broadcast([128, 2, 128]),
                     ALU.mult, reads=Cm_bl + [psb_b[6]], writes=[prodm_b])
                S.op("dve", lambda e, s=s: e.tensor_reduce(out=scm[:, s * 4:(s + 1) * 4],
                                                        in_=prodm.rearrange("p m (h d) -> p (m h) d", h=2),
                                                        axis=AX.X, op=ALU.add), reads=[prodm_b], writes=[scm_b])
            P.act(pTsm, scm, AF.Exp, reads=[scm_b], writes=[pTsm_b])
            for s in range(NS):
                for hh in range(2):
                    hr = slice(64 * hh, 64 * hh + 64)
                    for mt in range(2):
                        rhs = pTsm[:, s * 4 + mt * 2 + hh:s * 4 + mt * 2 + hh + 1]
                        P.mm(psb[4][hr, s:s + 1], Cm[:, s, mt, 1, hh * 64:(hh + 1) * 64], rhs, mt == 0, mt == 1,
                             reads=Cm_bl + [pTsm_b], writes=[psb_b[4]])
                    for mt in range(2):
                        rhs = pTsm[:, s * 4 + mt * 2 + hh:s * 4 + mt * 2 + hh + 1]
                        P.mm(psb[5][hr, s:s + 1], ones[:, 0:64], rhs, mt == 0, mt == 1, reads=[const_b, pTsm_b], writes=[psb_b[5]])
            P.act(lnl[0][:, 0:NS], psb[5][:, 0:NS], AF.Ln, reads=[psb_b[5]], writes=[lnl_b[0]])
            P.act(lnl[0][:, 0:NS], lnl[0][:, 0:NS], AF.Exp, reads=[lnl_b[0]], writes=[lnl_b[0]], scale=-1.0)
            P.tt("dve", tO[0][:, 0:NS], lnl[0][:, 0:NS], psb[4][:, 0:NS], ALU.mult, reads=[lnl_b[0], psb_b[4]], writes=[tO_b[0]])
            P.tt("dve", umT[:, hp, NT:NTS], tO[0][:, 0:NS], gmT[:, hp, NT:NTS], ALU.mult, reads=[tO_b[0], gmT_b[4]], writes=[umT_b[hp]])
        S.barrier()
        dbg("umT", umT.rearrange("p a t -> p (a t)"), [128, 2 * NTS], umT_b)
        A.release(mM)

    ubT = A.alloc([4, NTS], BF16); ubT_b = [Buf("ubT%d" % i) for i in range(4)]
    umT = A.alloc([2, NTS], BF16); umT_b = [Buf("umT0"), Buf("umT1")]
    if "B" in phases:
        mB = A.mark()
        olocT = A.alloc([4, NTS], F32); oloc_b = [[Buf() for _ in range(5)] for _ in range(4)]
        qcumT = A.alloc([4, NT], BF16); qcum_b = [Buf() for _ in range(4)]
        Sfin = A.alloc([4, 128], F32); Sfin_b = Buf()
        Dtot = A.alloc([4], F32); Dtot_b = Buf()
        cmask = A.alloc([NT], BF16); cmask_b = Buf()
        omlb = A.alloc([4], F32); lbt = A.alloc([8], F32); lb_b = Buf()
        P.dma(lbt, lbl, reads=[], writes=[lb_b], slot=P.newslot("lb"))
        P.tt("dve", omlb, lbt[:, 0:4], lbt[:, 4:8], ALU.subtract, reads=[lb_b], writes=[lb_b])
        P.act(omlb, omlb, AF.Sigmoid, reads=[lb_b], writes=[lb_b], scale=-1.0)
        ghg_t = A.alloc([4], F32); ghg_b = Buf()
        P.dma(ghg_t, ghg, reads=[], writes=[ghg_b], slot=P.newslot("gh"))
        P.memset("pool", cmask, 1.0, writes=[cmask_b])
        P.memset("pool", cmask.rearrange("p (c t) -> p c t", t=64)[:, :, 0:1], 0.0, writes=[cmask_b])
        Wb = A.alloc([KC, 3, 128], BF16); Wb_b = [Buf() for _ in range(3)]; Wb_s = [P.newslot("wb") for _ in range(3)]
        sold_s2 = [P.newslot("so"), P.newslot("so")]; shg_s2 = [P.newslot("sg"), P.newslot("sg")]
        mH = A.mark()
        for h in range(1):
            hq = A.alloc([NTS], BF16); hq_b = [Buf() for _ in range(5)]
            hq32s = A.alloc([NS], F32); ks32 = A.alloc([NS], F32); fs32 = A.alloc([NS], F32); ksbf = A.alloc([NS], BF16)
            smp_b = Buf()
            kbf = A.alloc([NT], BF16); kbf_b = Buf()
            gT = A.alloc([NT], F32); gT_b = Buf()
            GT = A.alloc([NT], F32); GT_b = Buf()
            Gs = A.alloc([32], F32); gs_b = Buf()
            qtT = A.alloc([NT], BF16); khT = A.alloc([NT], BF16); ktT = A.alloc([NT], BF16)
            qt_b, kh_b, kt_b = Buf(), Buf(), Buf()
            vtm = A.alloc([17, 128], BF16); vtm_b = [Buf() for _ in range(17)]
            ktm = A.alloc([16, 128], BF16); ktm_b = [Buf(), Buf()]
            eGl = A.alloc([32], F32); gsum = A.alloc([1], F32); eg_b = Buf()
            Sroll = [A.alloc([128], F32), A.alloc([128], F32)]; Sroll_b = [Buf(), Buf()]
            Sbf = A.alloc([33, 128], BF16); Sbf_b = [Buf() for _ in range(33)]
            attm = [A.alloc([64], BF16) for _ in range(4)]; attm_b = [Buf() for _ in range(4)]
            Sold2 = [A.alloc([NS // 2, 128], F32), A.alloc([NS // 2, 128], F32)]; Sold_b2 = [Buf(), Buf()]
            kstm = A.alloc([128], BF16); kstm_b = Buf()
            Vm = A.alloc([NS, 128], BF16); Vm_b = Buf()
            print("arena top phase B (KB):", A.top * 4 / 1024)
        def fm_b(c, c0, n, rd, evac):
            bank = next_bank()
            for kc in range(KC):
                P.mm(psb[bank][:, 0:n], Wb[:, kc, c, :], hT[:, kc, c0:c0 + n], kc == 0, kc == KC - 1,
                     reads=[Wb_b[c]] + rd, writes=[psb_b[bank]])
            evac(psb[bank][:, 0:n], psb_b[bank])

        def proj_main(h):
            wc = [2560 + h * 128, 3072 + h * 128]
            for c in range(2):
                P.dma(Wb[:, :, c, :], w_in_v[:, :, wc[c]:wc[c] + 128], reads=[], writes=[Wb_b[c]], slot=Wb_s[c], eng="pool")
            for tb in range(4):
                c0, n = tb * 512, 512
                fm_b(0, c0, n, [hT_b[tb]], lambda ps, pb, c0=c0, n=n, tb=tb: P.act(hq[:, c0:c0 + n], ps, AF.Silu, reads=[pb], writes=[hq_b[tb]]))
            for tb in range(4):
                c0, n = tb * 512, 512
                fm_b(1, c0, n, [hT_b[tb]], lambda ps, pb, c0=c0, n=n: P.act(GT[:, c0:c0 + n], ps, AF.Sigmoid, reads=[pb], writes=[GT_b], scale=-1.0))

        proj_main(0)
        for h in range(4):
            P.dma(Wb[:, :, 2, :], w_in_v[:, :, 3584 + h * 128:3584 + (h + 1) * 128], reads=[], writes=[Wb_b[2]], slot=Wb_s[2], eng="pool")
            fm_b(0, NT, NS, [hT_b[4]], lambda ps, pb: P.act(hq[:, NT:NTS], ps, AF.Silu, reads=[pb], writes=[hq_b[4]]))
            fm_b(0, NT, NS, [hT_b[4]], lambda ps, pb: P.act(hq32s, ps, AF.Silu, reads=[pb], writes=[smp_b]))
            fm_b(1, NT, NS, [hT_b[4]], lambda ps, pb: P.act(ks32, ps, AF.Sigmoid, reads=[pb], writes=[smp_b], scale=-1.0))
            P.ts("dve", GT, GT, omlb[:, h:h + 1], None, ALU.mult, None, reads=[GT_b, lb_b], writes=[GT_b])
            P.act(gT, GT, AF.Ln, reads=[GT_b], writes=[gT_b], scale=-1.0, bias=1.0)
            P.copy("pool", kbf, GT, reads=[GT_b], writes=[kbf_b])
            P.ts("dve", ks32, ks32, omlb[:, h:h + 1], None, ALU.mult, None, reads=[smp_b, lb_b], writes=[smp_b])
            P.ts("dve", fs32, ks32, -1.0, 1.0, ALU.mult, ALU.add, reads=[smp_b], writes=[smp_b])
            P.copy("dve", ksbf, ks32, reads=[smp_b], writes=[smp_b])
            for t0 in range(0, 17, 4):
                bank = next_bank()
                nt_ = min(4, 17 - t0)
                for ti in range(nt_):
                    t = t0 + ti
                    lc, part, rb = (slice(t * 128, (t + 1) * 128), 128, hT_b[t // 4]) if t < 16 else (slice(NT, NTS), NS, hT_b[4])
                    for kc in range(KC):
                        P.mm(psb[bank][0:part, ti * 128:(ti + 1) * 128], hT[:, kc, lc], Wb[:, kc, 2, :], kc == 0, kc == KC - 1,
                             reads=[Wb_b[2], rb], writes=[psb_b[bank]])
                if t0 < 16:
                    P.copy("act", vtm[:, t0:t0 + 4, :], psb[bank][:, 0:512].rearrange("p (t d) -> p t d", d=128),
                           reads=[psb_b[bank]], writes=vtm_b[t0:t0 + 4])
                else:
                    P.copy("act", vtm[0:NS, 16, :], psb[bank][0:NS, 0:128], reads=[psb_b[bank]], writes=[vtm_b[16]])
            chk('b_proj')
            S.op("dve", lambda e, GT=GT, gT=gT: e.tensor_tensor_scan(out=GT, data0=cmask, data1=gT, initial=0.0, op0=ALU.mult, op1=ALU.add),
                 reads=[cmask_b, gT_b, kbf_b], writes=[GT_b])
            GT3 = GT.rearrange("p (c t) -> p c t", t=64)
            P.act(gT, GT, AF.Exp, reads=[GT_b, gT_b], writes=[gT_b], scale=-1.0)
            P.tt("dve", khT, kbf, gT, ALU.mult, reads=[kbf_b, gT_b], writes=[kh_b])
            P.act(eGl, GT3[:, :, 63], AF.Exp, reads=[GT_b], writes=[eg_b])
            P.tt("dve", ktT.rearrange("p (c t) -> p c t", t=64), khT.rearrange("p (c t) -> p c t", t=64),
                 eGl.rearrange("p (c o) -> p c o", o=1).to_broadcast([128, 32, 64]), ALU.mult, reads=[kh_b, eg_b], writes=[kt_b])
            P.act(gT, GT, AF.Exp, reads=[GT_b, gT_b, kh_b], writes=[gT_b])
            P.tt("dve", qtT, hq[:, 0:NT], gT, ALU.mult, reads=hq_b[0:4] + [gT_b], writes=[qt_b])
            S.op("dve", lambda e, GT3=GT3, Gs=Gs: e.tensor_tensor_scan(out=Gs, data0=ones[:, 0:1].to_broadcast([128, 32]), data1=GT3[:, :, 63],
                                                                initial=0.0, op0=ALU.mult, op1=ALU.add), reads=[const_b, GT_b], writes=[gs_b])
            P.act(Dtot[:, h:h + 1], Gs[:, 31:32], AF.Exp, reads=[gs_b], writes=[Dtot_b])
            P.tt("dve", Gs, Gs, GT3[:, :, 63], ALU.subtract, reads=[gs_b, GT_b], writes=[gs_b])
            P.act(Gs, Gs, AF.Exp, reads=[gs_b], writes=[gs_b])
            P.tt("dve", qcumT[:, h, :].rearrange("p (c t) -> p c t", t=64), qtT.rearrange("p (c t) -> p c t", t=64),
                 Gs.rearrange("p (c o) -> p c o", o=1).to_broadcast([128, 32, 64]), ALU.mult, reads=[qt_b, gs_b], writes=[qcum_b[h]])
            chk('b_decay')
            if h < 3:
                proj_main(h + 1)
            for half in range(2):
                bank = next_bank()
                pstb = psb[bank].bitcast(BF16)
                for i in range(8):
                    tp = half * 8 + i
                    P.transpose(pstb[:, i * 128:(i + 1) * 128], ktT[:, tp * 128:(tp + 1) * 128], ident, reads=[kt_b, const_b], writes=[psb_b[bank]])
                P.copy("act", ktm[:, half * 8:half * 8 + 8, :], pstb.rearrange("p (t d) -> p t d", d=128), reads=[psb_b[bank]], writes=[ktm_b[half]])
            for hf in range(2):
                P.dma(Sold2[hf], sh[hf * (NS // 2):(hf + 1) * (NS // 2), h, :, :].rearrange("s k v -> k s v"), reads=[],
                      writes=[Sold_b2[hf]], slot=sold_s2[hf])
            chk('b_tr')
            pskv = [(psb[i][:, 0:128], psb_b[i]) for i in range(4)]
            P.memset("pool", Sroll[0], 0.0, writes=[Sroll_b[0]])
            P.memset("pool", Sbf[:, 0, :], 0.0, writes=[Sbf_b[0]])
            chk('b_bar')
            for c in range(32):
                if c in (1, 2, 3, 9):
                    chk('b_c%d' % c)
                tp, half = c // 2, c % 2
                rows = slice(64 * half, 64 * half + 64)
                ps, pb = pskv[c % 4]
                P.mm(ps, ktm[rows, tp, :], vtm[rows, tp, :], True, True, reads=[ktm_b[tp // 8], vtm_b[tp]], writes=[pb])
                P.stt(Sroll[(c + 1) % 2], Sroll[c % 2], eGl[:, c:c + 1], ps, ALU.mult, ALU.add,
                      reads=[Sroll_b[c % 2], eg_b, pb], writes=[Sroll_b[(c + 1) % 2]])
                P.copy("act", Sbf[:, c + 1, :], Sroll[(c + 1) % 2], reads=[Sroll_b[(c + 1) % 2]], writes=[Sbf_b[c + 1]])
            P.copy("dve", Sfin[:, h, :], Sroll[0], reads=[Sroll_b[0]], writes=[Sfin_b])
            chk('b_chain')
            psatt = [(psb[i][:, 0:64], psb_b[i]) for i in (0, 1, 6, 7)]
            pso = [[(psb[2], psb_b[2]), (psb[3], psb_b[3])], [(psb[4], psb_b[4]), (psb[5], psb_b[5])]]

            def att_front(c):
                half = c % 2
                rows = slice(64 * half, 64 * half + 64)
                cc = slice(c * 64, (c + 1) * 64)
                pa, pab = psatt[c % 4]
                ai = c % 4
                P.mm(pa[rows, :], khT[:, cc], qtT[:, cc], True, True, reads=[kh_b, qt_b], writes=[pab])
                P.tt("dve", attm[ai][rows, :], pa[rows, :], caus[rows, :], ALU.mult, reads=[pab, const_b], writes=[attm_b[ai]])

            def att_back(c):
                tp, half = c // 2, c % 2
                rows = slice(64 * half, 64 * half + 64)
                cc = slice(c * 64, (c + 1) * 64)
                ai = c % 4
                po, pob = pso[(c // 16) % 2][c % 2]
                oc = slice(((c % 16) // 2) * 64, ((c % 16) // 2) * 64 + 64)
                P.mm(po[:, oc], vtm[rows, tp, :], attm[ai][rows, :], True, False, reads=[vtm_b[tp], attm_b[ai]], writes=[pob])
                P.mm(po[:, oc], Sbf[:, c, :], qtT[:, cc], False, True, reads=[Sbf_b[c], qt_b], writes=[pob])
                if c % 16 == 15:
                    j = c // 16
                    dst = olocT[:, h, j * 1024:(j + 1) * 1024].rearrange("p (i two t) -> p i two t", two=2, t=64)
                    for par in range(2):
                        po2, pob2 = pso[j % 2][par]
                        P.copy("act", dst[:, :, par, :], po2.rearrange("p (i t) -> p i t", t=64), reads=[pob2],
                               writes=[oloc_b[h][2 * j], oloc_b[h][2 * j + 1]])

            att_front(0)
            att_front(1)
            for c in range(32):
                if c + 2 < 32:
                    att_front(c + 2)
                att_back(c)
            chk('b_att')
            pstb = psb[5].bitcast(BF16)
            P.transpose(pstb[0:NS, 0:128], ksbf, ident, reads=[smp_b, const_b], writes=[psb_b[5]])
            P.copy("act", kstm[0:NS], pstb[0:NS, 0:128], reads=[psb_b[5]], writes=[kstm_b])
            P.tt("pool", Vm[0:NS], vtm[0:NS, 16:17, :].to_broadcast([NS, NS, 128]), sel[0:NS], ALU.mult,
                 reads=[vtm_b[16], const_b], writes=[Vm_b])
            for s0 in (0, NS // 2):
                Sold, Sold_b = Sold2[s0 // (NS // 2)], Sold_b2[s0 // (NS // 2)]
                shg_s = shg_s2[s0 // (NS // 2)]
                for s in range(s0, s0 + NS // 2):
                    ps, pb = pskv[s % 4]
                    P.mm(ps, kstm[0:NS], Vm[0:NS, s, :], True, True, reads=[kstm_b, Vm_b], writes=[pb])
                    P.stt(Sold[:, s - s0, :], Sold[:, s - s0, :], fs32[:, s:s + 1], ps, ALU.mult, ALU.add,
                          reads=[Sold_b, smp_b, pb], writes=[Sold_b])
                for s in range(s0, s0 + NS // 2):
                    P.mm(psb[6][:, s:s + 1], Sold[:, s - s0, :], hq32s[:, s:s + 1], True, True, reads=[Sold_b, smp_b], writes=[psb_b[6]])
                P.dma(shg_o[s0:s0 + NS // 2, h, :, :].rearrange("s k v -> k s v"), Sold, reads=[Sold_b], writes=[Buf()], slot=shg_s)
            P.copy("act", olocT[:, h, NT:NTS], psb[6][:, 0:NS], reads=[psb_b[6]], writes=[oloc_b[h][4]])
            if h == 0:
                dbg("hq", hq, [128, NTS], hq_b)
                dbg("GT", GT, [128, NT], [GT_b])
                dbg("qtT", qtT, [128, NT], [qt_b]); dbg("khT", khT, [128, NT], [kh_b]); dbg("ktT", ktT, [128, NT], [kt_b])
                dbg("vtm", vtm.rearrange("p t d -> p (t d)"), [128, 17 * 128], vtm_b)
                dbg("Sbf", Sbf.rearrange("p c d -> p (c d)"), [128, 33 * 128], Sbf_b)
                dbg("oloc", olocT[:, 0, :], [128, NTS], oloc_b[0])
        S.barrier()
        A.release(mH)
        chk('b_heads')
        xin = nc.dram_tensor("xin", [4, 128, 129], F32)
        xout = nc.dram_tensor("xout", [16, 128, 129], F32)
        xin_b, xout_b = Buf(), Buf()
        xin_v = xin.ap().rearrange("h p c -> p h c")
        cc_slot = S.slot("cc", inc=1)

        def exchange():
            P.dma(xin_v[:, :, 0:128], Sfin, reads=[Sfin_b], writes=[xin_b], slot=P.newslot("xi"))
            P.dma(xin_v[:, :, 128:129], Dtot.rearrange("p (h o) -> p h o", o=1), reads=[Dtot_b, xin_b], writes=[xin_b], slot=P.newslot("xi"), slow=True)
            S.op("pool", lambda e: e.collective_compute("AllGather", ALU.bypass, replica_groups=[[0, 1, 2, 3], [4, 5, 6, 7]],
                                                        ins=[xin.ap()], outs=[xout.ap()]),
                 reads=[xin_b], writes=[xout_b], slot=cc_slot)

        if "M" in phases:
            phase_M(hook=exchange)
        else:
            exchange()
        Gx = A.alloc([12, 129], F32); Gx_b = Buf()
        xout_v = xout.ap().rearrange("(r h) p c -> p r h c", h=4)
        for h in range(4):
            P.dma(Gx[:, h * 3:(h + 1) * 3, :], xout_v[:, 0:3, h, :], reads=[xout_b], writes=[Gx_b], slot=P.newslot("gx"))
        Wg = A.alloc([KC, 128], BF16); Wg_b = Buf(); Wg_s = P.newslot("wg")
        gbT = A.alloc([NTS], BF16); gbT_b = [Buf() for _ in range(5)]
        Sin = A.alloc([128], F32); tS = A.alloc([128], F32); Sin_bf = A.alloc([128], BF16); Sin_b = Buf()
        hgst = A.alloc([128], F32); hgst_b = Buf(); hg_s = P.newslot("hg")
        sq = [A.alloc([512], BF16) for _ in range(3)]; sq_b = [Buf() for _ in range(3)]
        rs = [A.alloc([512], F32) for _ in range(3)]; rs_b = [Buf() for _ in range(3)]
        print("arena top phase B2 (KB):", A.top * 4 / 1024)
        for h in range(4):
            P.dma(Wg, w_in_v[:, :, 4096 + h * 128:4096 + (h + 1) * 128], reads=[], writes=[Wg_b], slot=Wg_s, eng="pool")
            P.memset("pool", Sin, 0.0, writes=[Sin_b])
            for p in range(3):
                gs = Gx[:, h * 3 + p, :]
                P.stt(tS, Sin, gs[:, 128:129], gs[:, 0:128], ALU.mult, ALU.add, reads=[Sin_b, Gx_b], writes=[Sin_b])
                P.tt("dve", tS, tS, Sin, ALU.subtract, reads=[Sin_b], writes=[Sin_b])
                P.stt(Sin, tS, flags[:, p:p + 1], Sin, ALU.mult, ALU.add, reads=[Sin_b, const_b], writes=[Sin_b])
            if h == 0:
                dbg("Sin", Sin, [128, 128], [Sin_b])
            P.copy("act", Sin_bf, Sin, reads=[Sin_b], writes=[Sin_b])
            P.stt(hgst, Sin, Dtot[:, h:h + 1], Sfin[:, h, :], ALU.mult, ALU.add, reads=[Sin_b, Dtot_b, Sfin_b], writes=[hgst_b])
            P.dma(hg_o[h], hgst, reads=[hgst_b], writes=[Buf()], slot=hg_s)
            for tb in range(5):
                c0, n = (tb * 512, 512) if tb < 4 else (NT, NS)
                bank = next_bank()
                for kc in range(KC):
                    P.mm(psb[bank][:, 0:n], Wg[:, kc, :], hT[:, kc, c0:c0 + n], kc == 0, kc == KC - 1,
                         reads=[Wg_b, hT_b[tb]], writes=[psb_b[bank]])
                P.act(gbT[:, c0:c0 + n], psb[bank][:, 0:n], AF.Silu, reads=[psb_b[bank]], writes=[gbT_b[tb]])
            b2bank = {}

            def b2_s1(tb):
                c0, n = (tb * 512, 512) if tb < 4 else (NT, NS)
                k = tb % 3
                oT = olocT[:, h, c0:c0 + n]
                if tb < 4:
                    bank = next_bank()
                    P.mm(psb[bank][:, 0:n], Sin_bf, qcumT[:, h, c0:c0 + n], True, True, reads=[Sin_b, qcum_b[h]], writes=[psb_b[bank]])
                    P.tt("dve", oT, oT, psb[bank][:, 0:n], ALU.add, reads=[oloc_b[h][tb], psb_b[bank]], writes=[oloc_b[h][tb]])
                P.act(sq[k][:, 0:n], oT, AF.Square, reads=[oloc_b[h][tb]], writes=[sq_b[k]])
                bank = next_bank()
                b2bank[tb] = bank
                P.mm(psb[bank][:, 0:n], ones, sq[k][:, 0:n], True, True, reads=[const_b, sq_b[k]], writes=[psb_b[bank]])

            def b2_s2(tb):
                c0, n = (tb * 512, 512) if tb < 4 else (NT, NS)
                k = tb % 3
                oT = olocT[:, h, c0:c0 + n]
                bank = b2bank[tb]
                P.act(rs[k][:, 0:n], psb[bank][:, 0:n], AF.Ln, reads=[psb_b[bank]], writes=[rs_b[k]], scale=1.0 / 128, bias=EPS)
                P.act(rs[k][:, 0:n], rs[k][:, 0:n], AF.Exp, reads=[rs_b[k]], writes=[rs_b[k]], scale=-0.5)
                P.stt(rs[k][:, 0:n], oT, ghg_t[:, h:h + 1], rs[k][:, 0:n], ALU.mult, ALU.mult, reads=[oloc_b[h][tb], ghg_b, rs_b[k]], writes=[rs_b[k]])
                P.tt("dve", ubT[:, h, c0:c0 + n], rs[k][:, 0:n], gbT[:, c0:c0 + n], ALU.mult, reads=[rs_b[k], gbT_b[tb]], writes=[ubT_b[h]])

            b2_s1(0)
            for tb in range(5):
                if tb + 1 < 5:
                    b2_s1(tb + 1)
                b2_s2(tb)
        dbg("ubT", ubT.rearrange("p a t -> p (a t)"), [128, 4 * NTS], ubT_b)
        S.barrier()
        A.release(mB)

    if "M" in phases and "B" not in phases:
        phase_M()

    if "O" in phases:
        mergedT = A.alloc([KC, NTS], BF16); mg_b = [[Buf() for _ in range(5)] for _ in range(KC)]
        wo = A.alloc([KC, 1024], BF16); wo_b = Buf()
        P.dma(wo, w_out.rearrange("(k p) n -> p k n", p=128), reads=[], writes=[wo_b], slot=P.newslot("wo"), eng="pool")
        gfb = A.alloc([1024], F32); gfb_b = Buf()
        P.dma(gfb, gfin, reads=[], writes=[gfb_b], slot=P.newslot("gf"))
        Wz = [A.alloc([KC, 3, 128], BF16) for _ in range(2)]
        Wz_b = [[Buf() for _ in range(3)] for _ in range(2)]; Wz_s = [[P.newslot("wz") for _ in range(3)] for _ in range(2)]
        Wbr = [A.alloc([8, 128], BF16) for _ in range(2)]
        Wbr_b = [[Buf() for _ in range(3)] for _ in range(2)]; Wbr_s = [[P.newslot("wr") for _ in range(3)] for _ in range(2)]
        gts = [A.alloc([512], F32) for _ in range(6)]; gts_b = [Buf() for _ in range(6)]
        mt_ = [A.alloc([512], F32) for _ in range(6)]; mt_b = [Buf() for _ in range(6)]
        xst = [A.alloc([1024], F32), A.alloc([1024], F32)]; xst_b = [Buf(), Buf()]
        xst_s = [P.newslot("xl"), P.newslot("xl")]; yst_s = [P.newslot("ys"), P.newslot("ys")]
        junk = A.alloc([1024], BF16)
        print("arena top phase O (KB):", A.top * 4 / 1024)
        w_a_v = w_a.rearrange("(c p) n -> p c n", p=128); w_b_v = w_b.rearrange("(c p) n -> p c n", p=128)
        w_m_v = w_m.rearrange("(c p) n -> p c n", p=128)
        uT = [(uaT, uaT_b, 0), (uaT, uaT_b, 1), (ubT, ubT_b, 0), (ubT, ubT_b, 1), (ubT, ubT_b, 2), (ubT, ubT_b, 3),
              (umT, umT_b, 0), (umT, umT_b, 1)]
        brs = [(0, 2), (2, 6), (6, 8)]
        unit = 0
        def load_w(dmc):
            k = dmc % 2
            dc = slice(dmc * 128, (dmc + 1) * 128)
            for i in range(3):
                col = 5120 + i * 1024 + dmc * 128
                P.dma(Wz[k][:, :, i, :], w_in_v[:, :, col:col + 128], reads=[], writes=[Wz_b[k][i]], slot=Wz_s[k][i], eng="pool")
            P.dma(Wbr[k][:, 0:2, :], w_a_v[:, :, dc], reads=[], writes=[Wbr_b[k][0]], slot=Wbr_s[k][0], eng="pool")
            P.dma(Wbr[k][:, 2:6, :], w_b_v[:, :, dc], reads=[], writes=[Wbr_b[k][1]], slot=Wbr_s[k][1], eng="pool")
            P.dma(Wbr[k][:, 6:8, :], w_m_v[:, :, dc], reads=[], writes=[Wbr_b[k][2]], slot=Wbr_s[k][2], eng="pool")

        load_w(0)
        for dmc in range(KC):
            k = dmc % 2
            if dmc + 1 < KC:
                load_w(dmc + 1)
            for tb in range(5):
                c0, n = (tb * 512, 512) if tb < 4 else (NT, NS)
                u3 = (unit % 2) * 3
                unit += 1
                for i in range(3):
                    bank = next_bank()
                    for kc in range(KC):
                        P.mm(psb[bank][:, 0:n], Wz[k][:, kc, i, :], hT[:, kc, c0:c0 + n], kc == 0, kc == KC - 1,
                             reads=[Wz_b[k][i], hT_b[tb]], writes=[psb_b[bank]])
                    P.act(gts[u3 + i][:, 0:n], psb[bank][:, 0:n], AF.Sigmoid, reads=[psb_b[bank]], writes=[gts_b[u3 + i]])
                for i in range(3):
                    bank = next_bank()
                    lo, hi = brs[i]
                    for c in range(lo, hi):
                        ut, ub_, ci = uT[c]
                        P.mm(psb[bank][:, 0:n], Wbr[k][:, c, :], ut[:, ci, c0:c0 + n], c == lo, c == hi - 1,
                             reads=[Wbr_b[k][i], ub_[ci]], writes=[psb_b[bank]])
                    P.tt("dve", mt_[u3 + i][:, 0:n], gts[u3 + i][:, 0:n], psb[bank][:, 0:n], ALU.mult,
                         reads=[gts_b[u3 + i], psb_b[bank]], writes=[mt_b[u3 + i]])
                P.tt("pool", mt_[u3][:, 0:n], mt_[u3][:, 0:n], mt_[u3 + 1][:, 0:n], ALU.add, reads=[mt_b[u3], mt_b[u3 + 1]], writes=[mt_b[u3]])
                P.tt("pool", mergedT[:, dmc, c0:c0 + n], mt_[u3][:, 0:n], mt_[u3 + 2][:, 0:n], ALU.add,
                     reads=[mt_b[u3], mt_b[u3 + 2]], writes=[mg_b[dmc][tb]])
        dbg("mergedT", mergedT[:, 0, :], [128, NTS], mg_b[0])
        xst3 = xst + [A.alloc([1024], F32)]; xst3_b = xst_b + [Buf()]
        xl_s = xst_s + [P.newslot("xl")]; ys_s = yst_s + [P.newslot("ys")]
        tinfo = []
        for t in range(17):
            if t < 16:
                tinfo.append((xo[t * 128:(t + 1) * 128, :], y_o[t * 128:(t + 1) * 128, :], 128, slice(t * 128, (t + 1) * 128), t // 4))
            else:
                tinfo.append((xs, ys_o, NS, slice(NT, NTS), 4))
        tj = {}

        def o_s1(t):
            src, dst, part, lc, tb = tinfo[t]
            k = t % 3
            tj[t] = P.tilectr
            P.tilectr += 1
            P.dma(xst3[k][0:part], src, reads=[], writes=[xst3_b[k]], slot=xl_s[k])
            for half in range(2):
                bank = next_bank()
                for kc in range(KC):
                    P.mm(psb[bank][0:part, :], mergedT[:, kc, lc], wo[:, kc, half * 512:(half + 1) * 512], kc == 0, kc == KC - 1,
                         reads=[mg_b[kc][tb], wo_b], writes=[psb_b[bank]])
                P.tt("dve", xst3[k][0:part, half * 512:(half + 1) * 512], xst3[k][0:part, half * 512:(half + 1) * 512], psb[bank][0:part, :],
                     ALU.add, reads=[xst3_b[k], psb_b[bank]], writes=[xst3_b[k]])

        def o_s2(t):
            src, dst, part, lc, tb = tinfo[t]
            k = t % 3
            j = tj[t]
            P.act(junk[0:part], xst3[k][0:part], AF.Square, reads=[xst3_b[k]], writes=[st_b[j]], accum_out=ssq[0:part, j:j + 1])
            P.act(lnv[0:part, j:j + 1], ssq[0:part, j:j + 1], AF.Ln, reads=[st_b[j]], writes=[st_b[j]], scale=1.0 / 1024, bias=EPS)
            P.act(rstd[0:part, j:j + 1], lnv[0:part, j:j + 1], AF.Exp, reads=[st_b[j]], writes=[st_b[j]], scale=-0.5)
            P.stt(xst3[k][0:part], xst3[k][0:part], rstd[0:part, j:j + 1], gfb[0:part], ALU.mult, ALU.mult,
                  reads=[xst3_b[k], st_b[j], gfb_b], writes=[xst3_b[k]])
            P.dma(dst, xst3[k][0:part], reads=[xst3_b[k]], writes=[Buf()], slot=ys_s[k])

        o_s1(0)
        for t in range(17):
            if t + 1 < 17:
                o_s1(t + 1)
            o_s2(t)


def _consts(q):
    k = np.arange(128)[:, None]
    qi = np.arange(128)[None, :]
    band = np.concatenate([(k >= qi), (k <= qi)], axis=1).astype(np.float32)
    bandh = band.copy()
    if q == 0:
        bandh[:, 0:128] = 0.0
    caus = (np.arange(64)[None, :] >= (np.arange(128)[:, None] % 64)).astype(np.float32)
    blk = np.zeros((128, 128), np.float32)
    blk[0:64, 0:64] = 1.0
    blk[64:128, 64:128] = 1.0
    sel = np.zeros((16, 16, 128), np.float32)
    for s in range(16):
        sel[s, s, :] = 1.0
    flags = np.zeros((128, 4), np.float32)
    for p in range(3):
        flags[:, p] = 1.0 if p < q else 0.0
    return {"c_ident": np.eye(128, dtype=np.float32), "c_band": band, "c_bandh": bandh, "c_caus": caus,
            "c_blk": blk, "c_sel": sel.reshape(16, 2048), "c_flags": flags}


def make_in_maps(inp):
    f = lambda a: np.ascontiguousarray(np.asarray(a, dtype=np.float32))
    xp = f(inp["x_prompt"]); xsm = f(inp["x_sample"])[:, 0, :]
    memp = f(inp["mem_prompt"])
    cws = [f(inp["cache_w1_kv"])[0], f(inp["cache_w2_kv"])[0], f(inp["cache_w3_kv"])[0]]
    cmem = f(inp["cache_mem_kv"])[0]; sh = f(inp["state_hgrn"])[0]
    lbl = f(inp["lb_logits"])
    lbl_t = np.ascontiguousarray(lbl.reshape(2, 4, 128).transpose(2, 0, 1)).reshape(128, 8)
    ghg = np.ascontiguousarray(f(inp["norm_hgrn"])[0].reshape(4, 128).T)
    shared = {
        "w_in": f(inp["w_in"])[0], "w_mem": f(inp["w_mem_kv"])[0], "w_a": f(inp["w_branch_a"])[0],
        "w_b": f(inp["w_branch_b"])[0], "w_m": f(inp["w_branch_m"])[0], "w_out": f(inp["w_out"])[0],
        "gin": np.ascontiguousarray(np.tile(f(inp["norm_in"])[0][None, :], (128, 1))),
        "gmem": np.ascontiguousarray(np.tile(f(inp["norm_mem"])[0][None, :], (128, 1))),
        "gfin": np.ascontiguousarray(np.tile(f(inp["norm_final"])[None, :], (128, 1))),
        "lbl": lbl_t, "ghg": ghg,
    }
    maps = []
    for c in range(8):
        b, q = c // 4, c % 4
        m = dict(shared)
        m["xo"] = np.ascontiguousarray(xp[b, q * NT:(q + 1) * NT])
        m["xh"] = np.ascontiguousarray(xp[b, (q - 1) * NT:q * NT]) if q > 0 else np.zeros((NT, 1024), np.float32)
        m["xs"] = np.ascontiguousarray(xsm[c * NS:(c + 1) * NS])
        m["mem"] = np.ascontiguousarray(memp[b])
        for g in range(3):
            m["cw%d" % (g + 1)] = np.ascontiguousarray(cws[g][c * NS:(c + 1) * NS].reshape(NS, -1, 512))
        m["cmem"] = np.ascontiguousarray(cmem[c * NS:(c + 1) * NS].reshape(NS, 256, 512))
        m["sh"] = np.ascontiguousarray(sh[c * NS:(c + 1) * NS])
        m.update(_consts(q))
        maps.append(m)
    return maps


def assemble(results):
    y = np.zeros((2, 8192, 1024), np.float32)
    ys = np.zeros((128, 1, 1024), np.float32)
    w1 = np.zeros((1, 2, 128, 2, 4, 64), np.float32)
    w2 = np.zeros((1, 2, 512, 2, 4, 64), np.float32)
    w3 = np.zeros((1, 2, 2048, 2, 4, 64), np.float32)
    mk = np.zeros((1, 2, 256, 2, 4, 64), np.float32)
    hg = np.zeros((1, 2, 4, 128, 128), np.float32)
    s1 = np.zeros((1, 128, 1, 2, 4, 64), np.float32)
    s2 = np.zeros_like(s1); s3 = np.zeros_like(s1)
    shg = np.zeros((1, 128, 4, 128, 128), np.float32)
    for c in range(8):
        r = results[c]
        b, q = c // 4, c % 4
        y[b, q * NT:(q + 1) * NT] = r["y"]
        ys[c * NS:(c + 1) * NS, 0] = r["ys"]
        s1[0, c * NS:(c + 1) * NS, 0] = r["skv1"].reshape(NS, 2, 4, 64)
        s2[0, c * NS:(c + 1) * NS, 0] = r["skv2"].reshape(NS, 2, 4, 64)
        s3[0, c * NS:(c + 1) * NS, 0] = r["skv3"].reshape(NS, 2, 4, 64)
        shg[0, c * NS:(c + 1) * NS] = r["shg"]
        if q == 3:
            w1[0, b] = r["kv1"].reshape(128, 2, 4, 64)
            w2[0, b] = r["kv2"].reshape(512, 2, 4, 64)
            w3[0, b] = r["kv3"].reshape(2048, 2, 4, 64)
            hg[0, b] = r["hg"]
        if q == 0:
            mk[0, b] = r["memkv"].reshape(256, 2, 4, 64)
    return (y, ys, w1, w2, w3, mk, hg, s1, s2, s3, shg)


_PROG = [None]


def kernel(**inputs):
    if _PROG[0] is None:
        _PROG[0] = build()
    P = _PROG[0]
    maps = make_in_maps(inputs)
    res = run_bass_kernel_spmd(P.nc, maps, core_ids=list(range(8)))
    return assemble(res.results)
```
